# Optimizing a Trainium2 kernel written in Bass

```python
import math
import jax, jax.numpy as jnp
from jax import lax
import numpy as np

D_MODEL = 1024
BATCH = 2
SEQ = 16384
DEPTH = 1

N_META = 16
D_MIX = D_MODEL
EPS = 1e-6
MLA_HEADS = 8
QK_NOPE = 64
QK_ROPE = 32
V_HEAD = 64
Q_LORA = 256
KV_LORA = 128
D_ATTN = MLA_HEADS * V_HEAD
ROPE_THETA = 10000.0
Q_BLOCK = 128
D_SSM = D_MIX - D_ATTN
SSM_GROUP = 16
N_SSM_GROUPS = D_SSM // SSM_GROUP
SSM_STATE = 64
DT_MIN = 1e-3
DT_MAX = 1e-1
IN_SPLITS = (Q_LORA, KV_LORA, QK_ROPE, D_ATTN, D_SSM, D_SSM)
D_IN = sum(IN_SPLITS)

kernel_name = "hymba_mla_s5_bidir_block"


def rmsnorm(x, w):
    xf = x.astype(jnp.float32)
    y = xf * lax.rsqrt(jnp.mean(xf * xf, axis=-1, keepdims=True) + EPS)
    return (y * w.astype(jnp.float32)).astype(x.dtype)


def rope(x, pos):
    d = x.shape[-1]
    half = d // 2
    inv = ROPE_THETA ** (-jnp.arange(half, dtype=jnp.float32) / half)
    ang = pos.astype(jnp.float32)[:, None] * inv[None, :]
    cos = jnp.cos(ang)[None, :, None, :]
    sin = jnp.sin(ang)[None, :, None, :]
    xf = x.astype(jnp.float32)
    x1, x2 = xf[..., :half], xf[..., half:]
    out = jnp.concatenate([x1 * cos - x2 * sin, x1 * sin + x2 * cos], axis=-1)
    return out.astype(x.dtype)


def block_attention(q, k, v):
    b, h, L, dqk = q.shape
    dv = v.shape[-1]
    n_blk = -(-L // Q_BLOCK)
    pad = n_blk * Q_BLOCK - L
    qp = jnp.pad(q, ((0, 0), (0, 0), (0, pad), (0, 0)))
    qb = qp.reshape(b, h, n_blk, Q_BLOCK, dqk).transpose(2, 0, 1, 3, 4)
    scale = 1.0 / math.sqrt(dqk)

    def one_block(qblk):
        s = jnp.einsum('bhqd,bhkd->bhqk', qblk, k).astype(jnp.float32) * scale
        p = jax.nn.softmax(s, axis=-1)
        return jnp.einsum('bhqk,bhkd->bhqd', p.astype(v.dtype), v)

    out = lax.map(one_block, qb)
    out = out.transpose(1, 0, 3, 2, 4).reshape(b, n_blk * Q_BLOCK, h * dv)
    return out[:, :L]


def s5_direction(u, a_re, a_im, log_dt, b_re, b_im, c_re, c_im, reverse):
    dt = jnp.exp(log_dt.astype(jnp.float32))[:, None]
    a_re = a_re.astype(jnp.float32)
    a_im = a_im.astype(jnp.float32)
    mag = jnp.exp(a_re * dt)
    abar_re = mag * jnp.cos(a_im * dt)
    abar_im = mag * jnp.sin(a_im * dt)
    num_re = abar_re - 1.0
    num_im = abar_im
    den = a_re * a_re + a_im * a_im
    coef_re = (num_re * a_re + num_im * a_im) / den
    coef_im = (num_im * a_re - num_re * a_im) / den
    b_re = b_re.astype(jnp.float32)
    b_im = b_im.astype(jnp.float32)
    bbar_re = coef_re[..., None] * b_re - coef_im[..., None] * b_im
    bbar_im = coef_re[..., None] * b_im + coef_im[..., None] * b_re
    bu_re = jnp.einsum('blgh,gph->blgp', u, bbar_re)
    bu_im = jnp.einsum('blgh,gph->blgp', u, bbar_im)
    L = u.shape[1]
    g, p = abar_re.shape
    as_re = jnp.broadcast_to(abar_re[None, None], (1, L, g, p))
    as_im = jnp.broadcast_to(abar_im[None, None], (1, L, g, p))

    def combine(left, right):
        ar1, ai1, br1, bi1 = left
        ar2, ai2, br2, bi2 = right
        ar = ar2 * ar1 - ai2 * ai1
        ai = ar2 * ai1 + ai2 * ar1
        br = ar2 * br1 - ai2 * bi1 + br2
        bi = ar2 * bi1 + ai2 * br1 + bi2
        return (ar, ai, br, bi)

    _, _, x_re, x_im = lax.associative_scan(
        combine, (as_re, as_im, bu_re, bu_im), axis=1, reverse=reverse)
    return (jnp.einsum('blgp,ghp->blgh', x_re, c_re.astype(jnp.float32))
            - jnp.einsum('blgp,ghp->blgh', x_im, c_im.astype(jnp.float32)))


def hybrid_layer(h, pos, pre_norm_w, post_norm_w, w_in, q_norm_w, w_q_up, kv_norm_w,
                 w_kv_up, attn_out_norm_w, ssm_a_re, ssm_a_im, ssm_log_dt, ssm_b_re,
                 ssm_b_im, ssm_c_re, ssm_c_im, ssm_d, w_glu, b_glu, ssm_out_norm_w, w_out):
    b, L, _ = h.shape
    xn = rmsnorm(h, pre_norm_w)
    proj = jnp.einsum('bld,de->ble', xn, w_in)
    offs = np.cumsum(IN_SPLITS)[:-1].tolist()
    q_lat, kv_lat, k_rope, attn_gate, ssm_u, ssm_gate = jnp.split(proj, offs, axis=-1)

    q = jnp.einsum('blr,re->ble', rmsnorm(q_lat, q_norm_w), w_q_up)
    q = q.reshape(b, L, MLA_HEADS, QK_NOPE + QK_ROPE)
    q_nope, q_rope = q[..., :QK_NOPE], rope(q[..., QK_NOPE:], pos)
    kv = jnp.einsum('blr,re->ble', rmsnorm(kv_lat, kv_norm_w), w_kv_up)
    kv = kv.reshape(b, L, MLA_HEADS, QK_NOPE + V_HEAD)
    k_nope, v = kv[..., :QK_NOPE], kv[..., QK_NOPE:]
    k_r = rope(k_rope[:, :, None, :], pos)
    k_r = jnp.broadcast_to(k_r, (b, L, MLA_HEADS, QK_ROPE))
    qf = jnp.concatenate([q_nope, q_rope], axis=-1).transpose(0, 2, 1, 3)
    kf = jnp.concatenate([k_nope, k_r], axis=-1).transpose(0, 2, 1, 3)
    vf = v.transpose(0, 2, 1, 3)
    y_attn = block_attention(qf, kf, vf)
    y_attn = rmsnorm(y_attn * jax.nn.silu(attn_gate), attn_out_norm_w)

    u = ssm_u.astype(jnp.float32).reshape(b, L, N_SSM_GROUPS, SSM_GROUP)
    y_f = s5_direction(u, ssm_a_re[0], ssm_a_im[0], ssm_log_dt[0], ssm_b_re[0],
                       ssm_b_im[0], ssm_c_re[0], ssm_c_im[0], reverse=False)
    y_b = s5_direction(u, ssm_a_re[1], ssm_a_im[1], ssm_log_dt[1], ssm_b_re[1],
                       ssm_b_im[1], ssm_c_re[1], ssm_c_im[1], reverse=True)
    y_ssm = (y_f + y_b).reshape(b, L, D_SSM) + ssm_d.astype(jnp.float32) * ssm_u.astype(jnp.float32)
    y_ssm = jax.nn.gelu(y_ssm).astype(h.dtype)
    glu = jnp.einsum('ble,ef->blf', y_ssm, w_glu) + b_glu
    y_ssm = glu[..., :D_SSM] * jax.nn.sigmoid(glu[..., D_SSM:])
    y_ssm = rmsnorm(y_ssm * jax.nn.silu(ssm_gate), ssm_out_norm_w)

    y = jnp.concatenate([y_attn, y_ssm], axis=-1)
    y = jnp.einsum('ble,ed->bld', y, w_out)
    return h + rmsnorm(y, post_norm_w)


def setup_inputs(seed: int = 0) -> dict:
    key = jax.random.key(seed)
    ks = jax.random.split(key, 24)
    f32 = jnp.float32

    def nrm(k, shape, fan_in):
        return jax.random.normal(k, shape, f32) * (fan_in ** -0.5)

    def gain(k, shape):
        return 1.0 + 0.02 * jax.random.normal(k, shape, f32)

    G, P, H = N_SSM_GROUPS, SSM_STATE, SSM_GROUP
    a_re = -0.5 + 0.01 * jax.random.normal(ks[10], (DEPTH, 2, G, P), f32)
    a_im = (jnp.pi * jnp.arange(P, dtype=f32))[None, None, None, :] \
        + 0.01 * jax.random.normal(ks[11], (DEPTH, 2, G, P), f32)
    log_dt = jax.random.uniform(ks[12], (DEPTH, 2, G), f32,
                                minval=math.log(DT_MIN), maxval=math.log(DT_MAX))
    return {
        "x": jax.random.normal(ks[0], (BATCH, SEQ, D_MODEL), f32),
        "meta_tokens": jax.random.normal(ks[1], (N_META, D_MODEL), f32),
        "pre_norm_w": gain(ks[2], (DEPTH, D_MODEL)),
        "post_norm_w": gain(ks[3], (DEPTH, D_MODEL)),
        "w_in": nrm(ks[4], (DEPTH, D_MODEL, D_IN), D_MODEL),
        "q_norm_w": gain(ks[5], (DEPTH, Q_LORA)),
        "w_q_up": nrm(ks[6], (DEPTH, Q_LORA, MLA_HEADS * (QK_NOPE + QK_ROPE)), Q_LORA),
        "kv_norm_w": gain(ks[7], (DEPTH, KV_LORA)),
        "w_kv_up": nrm(ks[8], (DEPTH, KV_LORA, MLA_HEADS * (QK_NOPE + V_HEAD)), KV_LORA),
        "attn_out_norm_w": gain(ks[9], (DEPTH, D_ATTN)),
        "ssm_a_re": a_re,
        "ssm_a_im": a_im,
        "ssm_log_dt": log_dt,
        "ssm_b_re": nrm(ks[13], (DEPTH, 2, G, P, H), 2 * H),
        "ssm_b_im": nrm(ks[14], (DEPTH, 2, G, P, H), 2 * H),
        "ssm_c_re": nrm(ks[15], (DEPTH, 2, G, H, P), 2 * P),
        "ssm_c_im": nrm(ks[16], (DEPTH, 2, G, H, P), 2 * P),
        "ssm_d": jax.random.normal(ks[17], (DEPTH, D_SSM), f32),
        "w_glu": nrm(ks[18], (DEPTH, D_SSM, 2 * D_SSM), D_SSM),
        "b_glu": 0.01 * jax.random.normal(ks[19], (DEPTH, 2 * D_SSM), f32),
        "ssm_out_norm_w": gain(ks[20], (DEPTH, D_SSM)),
        "w_out": nrm(ks[21], (DEPTH, D_MIX, D_MODEL), D_MIX),
    }


def reference(x, meta_tokens, pre_norm_w, post_norm_w, w_in, q_norm_w, w_q_up, kv_norm_w,
              w_kv_up, attn_out_norm_w, ssm_a_re, ssm_a_im, ssm_log_dt, ssm_b_re, ssm_b_im,
              ssm_c_re, ssm_c_im, ssm_d, w_glu, b_glu, ssm_out_norm_w, w_out):
    b = x.shape[0]
    meta = jnp.broadcast_to(meta_tokens[None].astype(x.dtype), (b, N_META, x.shape[-1]))
    h = jnp.concatenate([meta, x], axis=1)
    pos = jnp.arange(h.shape[1], dtype=jnp.int32)
    for i in range(DEPTH):
        h = hybrid_layer(h, pos, pre_norm_w[i], post_norm_w[i], w_in[i], q_norm_w[i],
                         w_q_up[i], kv_norm_w[i], w_kv_up[i], attn_out_norm_w[i],
                         ssm_a_re[i], ssm_a_im[i], ssm_log_dt[i], ssm_b_re[i], ssm_b_im[i],
                         ssm_c_re[i], ssm_c_im[i], ssm_d[i], w_glu[i], b_glu[i],
                         ssm_out_norm_w[i], w_out[i])
    return h[:, N_META:]
```

```python
import math
import numpy as np
import ml_dtypes
import concourse.bass as bass
import concourse.mybir as mybir
from concourse.bass_utils import run_bass_kernel_spmd

F32 = mybir.dt.float32
BF16 = mybir.dt.bfloat16
ALU = mybir.AluOpType
AF = mybir.ActivationFunctionType

SAME_ENGINE_SYNC = True
EPS = 1e-6
TWO_PI = 6.283185307179586
MAGIC = 12582912.0

DBG = set()
STOP_AFTER = None
SINGLE_LAUNCH = True


class _Op:
    __slots__ = ("eng", "fn", "deps", "dma_key", "signal", "count", "dma_count", "idx")


class Prog:
    ENGS = ("pe", "act", "dve", "pool", "sp")
    SYNC_SAME = ("act", "dve", "pool")

    def __init__(self, nc):
        self.nc = nc
        self.ops = []
        self.last_writer = {}
        self.readers = {}
        self.sb_off = 16512
        self.sb_top = 229344
        self.n_alloc = 0

    def sb(self, name, shape, dtype):
        nbytes = int(np.prod(shape[1:])) * (4 if dtype == F32 else 2)
        nbytes = (nbytes + 63) // 64 * 64
        assert self.sb_off + nbytes <= self.sb_top, (name, self.sb_off, nbytes)
        self.n_alloc += 1
        t = self.nc.alloc_sbuf_tensor_at(f"{name}_{self.n_alloc}", list(shape), dtype, offset=self.sb_off)
        self.sb_off += nbytes
        return t

    def mark(self):
        return self.sb_off

    def release(self, m):
        self.sb_off = m
        self.barrier()

    def _deps(self, reads, writes, eng=None):
        deps = set()
        for k in list(reads) + list(writes):
            w = self.last_writer.get(k)
            if w is not None:
                deps.add(w)
        for k in reads:
            if k.startswith("pb"):
                for r in self.readers.get(k, ()):
                    if self.ops[r].eng != eng:
                        deps.add(r)
        for k in writes:
            for r in self.readers.get(k, ()):
                deps.add(r)
        return deps

    def _commit(self, idx, reads, writes):
        for k in writes:
            self.last_writer[k] = idx
            self.readers[k] = []
        for k in reads:
            self.readers.setdefault(k, []).append(idx)

    def op(self, eng, fn, reads=(), writes=()):
        o = _Op()
        o.eng, o.fn, o.dma_key, o.signal = eng, fn, None, False
        o.idx = len(self.ops)
        o.deps = self._deps(reads, writes, eng)
        self.ops.append(o)
        self._commit(o.idx, reads, writes)
        return o.idx

    def dma(self, eng, out, in_, reads=(), writes=(), key=None):
        if key is None:
            key = writes[0]
        o = _Op()
        o.eng, o.dma_key, o.signal = eng, key, False
        o.fn = lambda e: e.dma_start(out=out, in_=in_)
        o.idx = len(self.ops)
        o.deps = self._deps(reads, writes)
        self.ops.append(o)
        self._commit(o.idx, reads, writes)
        return o.idx

    def barrier(self):
        last, lastdma = {}, {}
        for o in self.ops:
            if o.dma_key is not None:
                lastdma[o.dma_key] = o.idx
            elif o.fn is not None:
                last[o.eng] = o.idx
        deps = set(last.values()) | set(lastdma.values())
        for e in self.ENGS:
            o = _Op()
            o.eng, o.fn, o.dma_key, o.signal = e, None, None, False
            o.idx = len(self.ops)
            o.deps = set(deps)
            self.ops.append(o)
        self.last_writer = {}
        self.readers = {}

    def emit(self):
        nc, ops = self.nc, self.ops
        for o in ops:
            for d in o.deps:
                p = ops[d]
                if p.dma_key is None and (p.eng != o.eng or (SAME_ENGINE_SYNC and o.eng in self.SYNC_SAME)):
                    p.signal = True
        cnt = {e: 0 for e in self.ENGS}
        first, last = {}, {}
        barriers = []
        swq = set()
        for o in ops:
            if o.dma_key is not None:
                first.setdefault(o.dma_key, o.idx)
                last[o.dma_key] = o.idx
                if o.eng != "sp":
                    swq.add(o.dma_key)
            elif o.fn is None and o.eng == "sp":
                barriers.append(o.idx)
        import bisect
        pool = []
        slot_sem = {}
        nsem = 0
        for k in sorted(first, key=lambda k: first[k]):
            f = first[k]
            got = None
            for ent in (pool if k not in swq else ()):
                bi_ = bisect.bisect_right(barriers, ent[0])
                if bi_ < len(barriers) and barriers[bi_] < f:
                    got = ent
                    break
            if got is None:
                got = [last[k], nsem]
                nsem += 1
                if k not in swq:
                    pool.append(got)
            else:
                got[0] = last[k]
            slot_sem[k] = got[1]
        dcnt = {}
        for o in ops:
            if o.dma_key is not None:
                sid = slot_sem[o.dma_key]
                dcnt[sid] = dcnt.get(sid, 0) + 16
                o.dma_count = dcnt[sid]
            elif o.signal:
                cnt[o.eng] += 1
                o.count = cnt[o.eng]
        esem = {e: nc.alloc_semaphore(f"es_{e}") for e in self.ENGS}
        dsem_id = {sid: nc.alloc_semaphore(f"ds_{sid}") for sid in range(nsem)}
        dsem = {k: dsem_id[slot_sem[k]] for k in slot_sem}
        self.n_sems = len(esem) + nsem
        self.counts = dict(cnt)
        streams = {e: [] for e in self.ENGS}
        for o in ops:
            streams[o.eng].append(o)
        final_dma = dict(dcnt)
        dsem_final = dsem_id
        sync_same = self.SYNC_SAME

        self.n_inst = {e: 0 for e in self.ENGS}

        def run(eng_name, e):
            waited = {}
            for o in streams[eng_name]:
                need = {}
                for d in o.deps:
                    p = ops[d]
                    if p.dma_key is not None:
                        k, v = ("d", slot_sem[p.dma_key]), p.dma_count
                    else:
                        if p.eng == eng_name and not (SAME_ENGINE_SYNC and eng_name in sync_same):
                            continue
                        k, v = ("e", p.eng), p.count
                    if v > need.get(k, 0):
                        need[k] = v
                for k, v in need.items():
                    if waited.get(k, 0) >= v:
                        continue
                    waited[k] = v
                    e.wait_ge(dsem_id[k[1]] if k[0] == "d" else esem[k[1]], v)
                    self.n_inst[eng_name] += 1
                if o.fn is None:
                    continue
                ins = o.fn(e)
                self.n_inst[eng_name] += 1
                if o.dma_key is not None:
                    ins.then_inc(dsem[o.dma_key], 16)
                elif o.signal:
                    ins.then_inc(esem[o.eng], 1)
            if eng_name == "sp":
                for k, v in final_dma.items():
                    e.wait_ge(dsem_final[k], v)

        with nc.Block() as blk:
            @blk.tensor
            def _(e):
                run("pe", e)

            @blk.scalar
            def _(e):
                run("act", e)

            @blk.vector
            def _(e):
                run("dve", e)

            @blk.gpsimd
            def _(e):
                run("pool", e)

            @blk.sync
            def _(e):
                run("sp", e)


EVALS = np.array([7 - i for i in range(8)] + [i for i in range(8)] + [-i for i in range(8)]
                 + [j + 1 for j in range(8)] + [8 - j for j in range(8)] + [1, 8], np.float32)
NE = len(EVALS)


def _consts():
    c = {}
    c["ident"] = np.eye(128, dtype=np.float32)
    c["evals"] = np.tile(EVALS[None, :], (128, 1)).astype(np.float32)
    sg = np.ones((128, 4), np.float32)
    sg[:64, 0] = -1.0
    sg[:64, 1], sg[64:, 1] = 1.0, 0.0
    sg[:64, 2], sg[64:, 2] = 0.0, 1.0
    sg[:, 3] = -1.0
    c["sgn"] = sg
    ii = np.arange(128) // 16
    c["maskf"] = (ii[None, :] >= ii[:, None]).astype(np.float32)
    c["maskb"] = (ii[:, None] >= ii[None, :]).astype(np.float32)
    sel = np.zeros((8, 8, 128, 128), np.float32)
    for j in range(8):
        for g8 in range(8):
            for ho in range(16):
                sel[j, g8, 16 * j + ho, 16 * g8 + ho] = 1.0
    c["sel"] = np.ascontiguousarray(sel.reshape(64, 128, 128).transpose(1, 0, 2)).astype(ml_dtypes.bfloat16)
    sh = np.zeros((128, 64), np.float32)
    for r in range(64):
        sh[64 + r, r] = 1.0
    c["shsel"] = sh
    return c


def _rope_tables():
    half = 16
    inv = (10000.0 ** (-np.arange(half, dtype=np.float32) / np.float32(half))).astype(np.float32)
    return inv


def _key_positions():
    T, i, c = np.meshgrid(np.arange(16), np.arange(8), np.arange(128), indexing="ij")
    tok = (1024 * T + 8 * c + i).reshape(-1)
    pos_real = tok + 16
    i2, c2 = np.meshgrid(np.arange(8), np.arange(2), indexing="ij")
    pos_meta = (8 * c2 + i2).reshape(-1)
    return np.concatenate([pos_real, pos_meta]).astype(np.float32)


def _local_positions(q):
    i, c = np.meshgrid(np.arange(8), np.arange(512), indexing="ij")
    tok = (8 * c + i).reshape(-1) + 4096 * q
    return (tok + 16).astype(np.float32)


def _cos_sin(pos):
    inv = _rope_tables()
    ang = pos.astype(np.float32)[None, :] * inv[:, None]
    return np.cos(ang).astype(np.float32), np.sin(ang).astype(np.float32)


def _prep_shared(inp):
    f = lambda a: np.ascontiguousarray(np.asarray(a, dtype=np.float32))
    d = {}
    d["w_in_t"] = f(inp["w_in"][0].reshape(8, 128, 1952).transpose(1, 0, 2))
    d["pre_w"] = f(inp["pre_norm_w"][0].reshape(8, 128).T)
    d["wq_t"] = f(inp["w_q_up"][0].reshape(2, 128, 768).transpose(1, 0, 2))
    d["qn_w"] = f(inp["q_norm_w"][0].reshape(2, 128).T)
    d["wkv"] = f(inp["w_kv_up"][0])
    d["kvn_w"] = f(inp["kv_norm_w"][0].reshape(128, 1))
    d["wglu_t"] = f(inp["w_glu"][0].reshape(4, 128, 1024).transpose(1, 0, 2))
    d["bglu"] = f(inp["b_glu"][0].reshape(8, 128).T)
    d["wout_t"] = f(inp["w_out"][0].reshape(8, 128, 1024).transpose(1, 0, 2))
    onw = np.concatenate([np.asarray(inp["attn_out_norm_w"][0]), np.asarray(inp["ssm_out_norm_w"][0])])
    d["on_w"] = f(onw.reshape(8, 128).T)
    d["post_w"] = f(np.tile(np.asarray(inp["post_norm_w"][0])[None, :], (128, 1)))
    d["meta"] = f(inp["meta_tokens"])
    a_re, a_im = np.asarray(inp["ssm_a_re"][0]), np.asarray(inp["ssm_a_im"][0])
    ldt = np.asarray(inp["ssm_log_dt"][0])
    b_re, b_im = np.asarray(inp["ssm_b_re"][0]), np.asarray(inp["ssm_b_im"][0])
    c_re, c_im = np.asarray(inp["ssm_c_re"][0]), np.asarray(inp["ssm_c_im"][0])
    dup = lambda a: np.concatenate([a, a], axis=0)
    d["are_n"] = f(dup(a_re.transpose(2, 0, 1)))
    d["aim_n"] = f(dup(a_im.transpose(2, 0, 1)))
    d["ldt_n"] = f(np.tile(ldt[None], (128, 1, 1)))
    bre_t, bim_t = b_re.transpose(2, 0, 1, 3), b_im.transpose(2, 0, 1, 3)
    d["x1_n"] = f(np.concatenate([bre_t, bim_t], 0))
    d["x2_n"] = f(np.concatenate([bim_t, bre_t], 0))
    cre_t, cim_t = c_re.transpose(3, 0, 1, 2), c_im.transpose(3, 0, 1, 2)
    d["y1_n"] = f(dup(cre_t))
    d["y2_n"] = f(dup(cim_t))
    sl = lambda a: a.reshape(2, 16, 2, 64).transpose(2, 3, 0, 1).reshape(128, 2, 16)
    d["are_s"] = f(sl(a_re))
    d["aim_s"] = f(sl(a_im))
    d["ldt_s"] = f(sl(np.tile(ldt[:, :, None], (1, 1, 64))))
    sl4 = lambda c: c.transpose(0, 1, 3, 2).reshape(2, 16, 2, 64, 16).transpose(2, 3, 0, 1, 4).reshape(128, 2, 16, 16)
    d["cs1"] = f(sl4(c_re))
    d["cs2"] = f(sl4(c_im))
    dd = np.asarray(inp["ssm_d"][0]).reshape(32, 16)
    d["drep"] = f(np.tile(dd.T[None], (8, 1, 1)).reshape(128, 32))
    d.update(_consts())
    ck, sk = _cos_sin(_key_positions())
    d["cosk"] = f(np.concatenate([ck, ck], 0))
    d["sink"] = f(np.concatenate([sk, sk], 0))
    return d


def _prep_core(inp, shared, core):
    b, q = core // 4, core % 4
    d = dict(shared)
    x = np.asarray(inp["x"], dtype=np.float32)
    d["x_all"] = np.ascontiguousarray(x[b])
    d["x_loc"] = np.ascontiguousarray(x[b, 4096 * q:4096 * (q + 1)])
    sq = np.zeros((128, 4), np.float32)
    sq[:, q] = 1.0
    d["selq"] = sq
    cq, s_q = _cos_sin(_local_positions(q))
    sc = np.float32(1.0 / math.sqrt(96.0))
    c96 = np.ones((96, 4096), np.float32)
    s96 = np.zeros((96, 4096), np.float32)
    c96[64:80], c96[80:96] = cq, cq
    s96[64:80], s96[80:96] = s_q, s_q
    d["c96"] = np.ascontiguousarray(c96 * sc)
    d["s96"] = np.ascontiguousarray(s96 * sc)
    return d


class Ctx:
    pass


def _bc(ap, axis, shape):
    return ap.unsqueeze(axis).broadcast_to(list(shape))


def build(sample, dbg=(), stop=None, skip=()):
    nc = bass.Bass("TRN2", target_bir_lowering=False)
    P = Prog(nc)
    C = Ctx()
    C.nc, C.P, C.dbg, C.stop, C.skip = nc, P, set(dbg), stop, set(skip)
    D = {}
    for name, arr in sample.items():
        dt = BF16 if arr.dtype == ml_dtypes.bfloat16 else F32
        D[name] = nc.dram_tensor(name, list(arr.shape), dt, kind="ExternalInput").ap()
    C.D = D
    C.out = nc.dram_tensor("out", [4096, 1024], F32, kind="ExternalOutput").ap()
    C.psall = nc.alloc_psum_tensor("psall", [128, 4096], F32)
    C.bank_i = 0
    C.dbg_names = []

    def op(eng, fn, reads=(), writes=()):
        return P.op(eng, fn, reads, writes)
    C.op = op

    def nb():
        i = C.bank_i
        C.bank_i = (i + 1) % 8
        return i
    C.nb = nb
    C.bank = lambda i: C.psall[:, 512 * i:512 * (i + 1)]
    C.bankb = lambda i: C.psall[:, 512 * i:512 * (i + 1)].bitcast(BF16)
    C.bk = lambda i: f"pb{i}"

    def dump(name, src_ap, shape, key, dt=F32):
        if name in C.dbg:
            P.barrier()
            o = nc.dram_tensor("dbg_" + name, list(shape), dt, kind="ExternalOutput").ap()
            P.dma("sp", o, src_ap, reads=[key], writes=["dbg_" + name])
            C.dbg_names.append("dbg_" + name)
    C.dump = dump

    def mm(out, lhsT, rhs, start, stop_, reads, writes):
        op("pe", lambda e: e.matmul(out, lhsT, rhs, start=start, stop=stop_), reads, writes)

    def tr(out, in_, ident, reads, writes):
        op("pe", lambda e: e.transpose(out, in_, ident), reads, writes)

    def act(out, in_, func, reads, writes, scale=1.0, bias=None, accum=None):
        if bias is None and accum is None:
            op("act", lambda e: e.activation(out, in_, func, scale=scale), reads, writes)
        elif accum is None:
            op("act", lambda e: e.activation(out, in_, func, bias=bias, scale=scale), reads, writes)
        elif bias is None:
            op("act", lambda e: e.activation(out, in_, func, scale=scale, accum_out=accum), reads, writes)
        else:
            op("act", lambda e: e.activation(out, in_, func, bias=bias, scale=scale, accum_out=accum), reads, writes)

    def tt(eng, out, a, b, alu, reads, writes):
        op(eng, lambda e: e.tensor_tensor(out, a, b, alu), reads, writes)

    def ts(eng, out, a, s1, s2, op0, op1, reads, writes):
        if s2 is None:
            op(eng, lambda e: e.tensor_single_scalar(out, a, s1, op0), reads, writes)
        else:
            op(eng, lambda e: e.tensor_scalar(out, a, s1, s2, op0, op1), reads, writes)

    def stt(out, a, scalar, b, op0, op1, reads, writes, accum=None):
        if accum is None:
            op("dve", lambda e: e.scalar_tensor_tensor(out, a, scalar, b, op0, op1), reads, writes)
        else:
            op("dve", lambda e: e.scalar_tensor_tensor(out, a, scalar, b, op0, op1, accum_out=accum), reads, writes)

    def cp(eng, out, in_, reads, writes):
        if eng == "act":
            op("act", lambda e: e.activation(out, in_, AF.Copy), reads, writes)
        else:
            op(eng, lambda e: e.tensor_copy(out, in_), reads, writes)

    def ms(eng, out, val, writes):
        op(eng, lambda e: e.memset(out, val), (), writes)

    def ld(out, in_, writes, reads=(), eng="sp"):
        P.dma(eng, out, in_, reads=reads, writes=writes)

    C.st_n = 0

    def st(out, in_, slot, reads):
        C.st_n += 1
        P.dma("pool", out, in_, reads=reads, writes=[f"{slot}#{C.st_n}"], key=slot)
    C.st = st

    C.mm, C.tr, C.act, C.tt, C.ts, C.stt, C.cp, C.ms, C.ld = mm, tr, act, tt, ts, stt, cp, ms, ld

    phase0(C)
    if stop == "p0":
        return finish(C)
    phase_global(C)
    if stop == "pg":
        return finish(C)
    phase_local(C)
    if stop and stop.startswith("pl"):
        return finish(C)
    phase_ssm(C)
    if stop == "ps":
        return finish(C)
    phase_attn(C)
    if stop == "pa":
        return finish(C)
    phase_out(C)
    return finish(C)


def finish(C):
    C.P.emit()
    return C.nc, C


def _reduce_ang(C, R, ANG, n, shift, key):
    ts = C.ts
    if shift != 0.0:
        ts("dve", R, ANG, shift, None, ALU.add, None, [key + "_a"], [key + "_r"])
        src = R
    else:
        src = ANG
    tmp = C.tmp_ang
    ts("dve", tmp, src, 1.0 / TWO_PI, MAGIC, ALU.mult, ALU.add, [key + "_a", key + "_r"], ["tmp_ang"])
    ts("dve", tmp, tmp, MAGIC, -TWO_PI, ALU.subtract, ALU.mult, ["tmp_ang"], ["tmp_ang"])
    C.tt("dve", R, tmp, src, ALU.add, ["tmp_ang", key + "_a", key + "_r"], [key + "_r"])


def _powers(C, name, are, aim, ldt, ncol, ecols, n_e):
    P, op, tt, ts, act = C.P, C.op, C.tt, C.ts, C.act
    dt = P.sb(name + "dt", [128, ncol], F32)
    lam = P.sb(name + "lam", [128, ncol], F32)
    th = P.sb(name + "th", [128, ncol], F32)
    act(dt[:], ldt, AF.Exp, [name + "ldt"], [name + "dt"])
    tt("dve", lam[:], are, dt[:], ALU.mult, [name + "are", name + "dt"], [name + "lam"])
    tt("dve", th[:], aim, dt[:], ALU.mult, [name + "aim", name + "dt"], [name + "th"])
    sh = [128, ncol, n_e]
    PWre = P.sb(name + "pwre", sh, F32)
    PWim = P.sb(name + "pwim", sh, F32)
    mtmp = P.mark()
    ANG = P.sb(name + "ang", sh, F32)
    LAM = P.sb(name + "LAM", sh, F32)
    R = P.sb(name + "R", sh, F32)
    C.tmp_ang = P.sb(name + "tmpang", sh, F32)[:]
    ev = C.evals[:, ecols[0]:ecols[1]]
    tt("dve", ANG[:], _bc(th[:], 2, sh), _bc(ev, 1, sh), ALU.mult, [name + "th", "evals"], [name + "_a"])
    tt("dve", LAM[:], _bc(lam[:], 2, sh), _bc(ev, 1, sh), ALU.mult, [name + "lam", "evals"], [name + "LAM"])
    act(LAM[:], LAM[:], AF.Exp, [name + "LAM"], [name + "LAM"])
    _reduce_ang(C, R[:], ANG[:], 0, 0.0, name)
    act(PWim[:], R[:], AF.Sin, [name + "_r"], [name + "pwim"])
    _reduce_ang(C, R[:], ANG[:], 0, math.pi / 2, name)
    act(PWre[:], R[:], AF.Sin, [name + "_r"], [name + "pwre"])
    tt("dve", PWre[:], PWre[:], LAM[:], ALU.mult, [name + "pwre", name + "LAM"], [name + "pwre"])
    tt("dve", PWim[:], PWim[:], LAM[:], ALU.mult, [name + "pwim", name + "LAM"], [name + "pwim"])
    P.release(mtmp)
    P.barrier()
    return PWre, PWim, lam, th


def phase0(C):
    P, D, op, mm, tr, act, tt, ts, stt, cp, ms, ld = C.P, C.D, C.op, C.mm, C.tr, C.act, C.tt, C.ts, C.stt, C.cp, C.ms, C.ld
    sb = P.sb
    C.ident_f = sb("ident_f", [128, 128], F32)
    C.ident_b = sb("ident_b", [128, 128], BF16)
    C.ones_f = sb("ones_f", [128, 128], F32)
    C.sgn = sb("sgn", [128, 4], F32)
    C.evals = sb("evals", [128, NE], F32)
    C.shsel = sb("shsel", [128, 64], F32)
    C.selq = sb("selq", [128, 4], F32)
    ld(C.ident_f[:], D["ident"][:, :], ["ident_f"])
    ld(C.sgn[:], D["sgn"][:, :], ["sgn"])
    ld(C.evals[:], D["evals"][:, :], ["evals"])
    ld(C.shsel[:], D["shsel"][:, :], ["shsel"])
    ld(C.selq[:], D["selq"][:, :], ["selq"])
    cp("dve", C.ident_b[:], C.ident_f[:], ["ident_f"], ["ident_b"])
    ms("pool", C.ones_f[:], 1.0, ["ones_f"])
    C.Mre = sb("Mre", [128, 32, 10], F32)
    C.Mim = sb("Mim", [128, 32, 10], F32)
    C.Mimn = sb("Mimn", [128, 32, 10], F32)
    C.pre_w = sb("pre_w", [128, 8], F32)
    C.npre_w = sb("npre_w", [128, 8], F32)
    C.accs = sb("accs", [128, 17, 16, 2, 4], F32)
    C.Xin = sb("Xin", [128, 2, 2, 16], F32)
    C.AXin = sb("AXin", [128, 2, 2, 16], F32)
    C.ssq = sb("ssq", [128, 4], F32)
    C.mBase = P.mark()
    C.Winb = sb("Winb", [128, 32, 2, 2, 128], BF16)
    C.Toepd = C.nc.dram_tensor("Toepd", [128, 32, 128], BF16).ap()
    C.Coutd = C.nc.dram_tensor("Coutd", [128, 2, 16, 2, 128], BF16).ap()
    ms("pool", C.Winb[:], 0.0, ["Winb"])
    m0 = P.mark()
    C.Toep = sb("Toep", [128, 32, 128], BF16)
    ncol = 64
    are = sb("are_n", [128, ncol], F32)
    aim = sb("aim_n", [128, ncol], F32)
    ldt = sb("ldt_n", [128, ncol], F32)
    ld(are[:], D["are_n"].rearrange("n d g -> n (d g)"), ["Nare"])
    ld(aim[:], D["aim_n"].rearrange("n d g -> n (d g)"), ["Naim"])
    ld(ldt[:], D["ldt_n"].rearrange("n d g -> n (d g)"), ["Nldt"])
    X1 = sb("x1", [128, 2, 32, 16], F32)
    X2 = sb("x2", [128, 2, 32, 16], F32)
    Y1 = sb("y1", [128, 2, 32, 16], F32)
    Y2 = sb("y2", [128, 2, 32, 16], F32)
    ld(X1[:], D["x1_n"][:, :, :, :], ["x1"])
    ld(X2[:], D["x2_n"][:, :, :, :], ["x2"])
    ld(Y1[:], D["y1_n"][:, :, :, :], ["y1"])
    ld(Y2[:], D["y2_n"][:, :, :, :], ["y2"])
    maskf = sb("maskf", [128, 128], F32)
    maskb = sb("maskb", [128, 128], F32)
    drep = sb("drep", [128, 32], F32)
    ld(maskf[:], D["maskf"][:, :], ["maskf"])
    ld(maskb[:], D["maskb"][:, :], ["maskb"])
    ld(drep[:], D["drep"][:, :], ["drep"])
    PWre, PWim, lam, th = _powers(C, "N", are[:], aim[:], ldt[:], ncol, (0, 41), 41)
    nre = sb("nre", [128, ncol], F32)
    den = sb("den", [128, ncol], F32)
    t1 = sb("t1", [128, ncol], F32)
    cre = sb("cre", [128, ncol], F32)
    cim = sb("cim", [128, ncol], F32)
    ts("dve", nre[:], PWre[:, :, 40], -1.0, None, ALU.add, None, ["Npwre"], ["nre"])
    tt("dve", den[:], are[:], are[:], ALU.mult, ["Nare"], ["den"])
    tt("dve", t1[:], aim[:], aim[:], ALU.mult, ["Naim"], ["t1"])
    tt("dve", den[:], den[:], t1[:], ALU.add, ["den", "t1"], ["den"])
    op("dve", lambda e: e.reciprocal(den[:], den[:]), ["den"], ["den"])
    tt("dve", cre[:], nre[:], are[:], ALU.mult, ["nre", "Nare"], ["cre"])
    tt("dve", t1[:], PWim[:, :, 40], aim[:], ALU.mult, ["Npwim", "Naim"], ["t1"])
    tt("dve", cre[:], cre[:], t1[:], ALU.add, ["cre", "t1"], ["cre"])
    tt("dve", cre[:], cre[:], den[:], ALU.mult, ["cre", "den"], ["cre"])
    tt("dve", cim[:], PWim[:, :, 40], are[:], ALU.mult, ["Npwim", "Nare"], ["cim"])
    tt("dve", t1[:], nre[:], aim[:], ALU.mult, ["nre", "Naim"], ["t1"])
    tt("dve", cim[:], cim[:], t1[:], ALU.subtract, ["cim", "t1"], ["cim"])
    tt("dve", cim[:], cim[:], den[:], ALU.mult, ["cim", "den"], ["cim"])
    shw = [128, ncol, 24]
    WFre = sb("WFre", shw, F32)
    WFim = sb("WFim", shw, F32)
    tw = sb("tw", shw, F32)
    tt("dve", WFre[:], PWre[:, :, 0:24], _bc(cre[:], 2, shw), ALU.mult, ["Npwre", "cre"], ["WFre"])
    tt("dve", tw[:], PWim[:, :, 0:24], _bc(cim[:], 2, shw), ALU.mult, ["Npwim", "cim"], ["tw"])
    tt("dve", WFre[:], WFre[:], tw[:], ALU.subtract, ["WFre", "tw"], ["WFre"])
    tt("dve", WFim[:], PWre[:, :, 0:24], _bc(cim[:], 2, shw), ALU.mult, ["Npwre", "cim"], ["WFim"])
    tt("dve", tw[:], PWim[:, :, 0:24], _bc(cre[:], 2, shw), ALU.mult, ["Npwim", "cre"], ["tw"])
    tt("dve", WFim[:], WFim[:], tw[:], ALU.add, ["WFim", "tw"], ["WFim"])
    ts("dve", WFim[:], WFim[:], C.sgn[:, 0:1], None, ALU.mult, None, ["WFim", "sgn"], ["WFim"])
    shr = [128, ncol, 16]
    RT1 = sb("RT1", shr, F32)
    RT2 = sb("RT2", shr, F32)
    tr2 = sb("tr2", shr, F32)
    ts("dve", RT1[:], PWre[:, :, 8:24], C.sgn[:, 1:2], None, ALU.mult, None, ["Npwre", "sgn"], ["RT1"])
    ts("dve", tr2[:], PWim[:, :, 8:24], C.sgn[:, 2:3], None, ALU.mult, None, ["Npwim", "sgn"], ["tr2"])
    tt("dve", RT1[:], RT1[:], tr2[:], ALU.subtract, ["RT1", "tr2"], ["RT1"])
    ts("dve", RT2[:], PWim[:, :, 8:24], C.sgn[:, 1:2], None, ALU.mult, None, ["Npwim", "sgn"], ["RT2"])
    ts("dve", tr2[:], PWre[:, :, 8:24], C.sgn[:, 2:3], None, ALU.mult, None, ["Npwre", "sgn"], ["tr2"])
    tt("dve", RT2[:], RT2[:], tr2[:], ALU.add, ["RT2", "tr2"], ["RT2"])
    ts("dve", RT2[:], RT2[:], -1.0, None, ALU.mult, None, ["RT2"], ["RT2"])

    shf = [128, 32, 8, 16]

    def fprod(name, s1, s2, A1, A2):
        F = sb(name, shf, F32)
        tt("dve", F[:], _bc(s1, 3, shf), _bc(A1, 2, shf), ALU.mult, ["WFre", "WFim", "RT1", "RT2", "x1", "y1"], [name])
        tt("pool", tf[:], _bc(s2, 3, shf), _bc(A2, 2, shf), ALU.mult, ["WFre", "WFim", "RT1", "RT2", "x2", "y2"], ["tf"])
        tt("dve", F[:], F[:], tf[:], ALU.add, [name, "tf"], [name])
        return F

    tf = sb("tf", shf, F32)

    def to_win(F, name, d):
        for g in range(32):
            b = C.nb()
            tr(C.bank(b)[:, 0:128], F[:, g].rearrange("n i h -> n (i h)"), C.ident_f[:], [name, "ident_f"], [C.bk(b)])
            g2 = g % 2
            dst = C.Winb[:, g, d, :, 64 * g2:64 * g2 + 64]
            src = C.bank(b)[:, 0:128].rearrange("p (r c) -> p r c", r=2)
            cp("act" if g % 2 == 0 else "dve", dst, src, [C.bk(b)], ["Winb"])

    Lb = fprod("Lb", WFre[:, 32:64, 8:16], WFim[:, 32:64, 8:16], X1[:, 1], X2[:, 1])
    to_win(Lb, "Lb", 1)
    Lf = fprod("Lf", WFre[:, 0:32, 0:8], WFim[:, 0:32, 0:8], X1[:, 0], X2[:, 0])
    to_win(Lf, "Lf", 0)
    tt("dve", Lf[:], _bc(WFre[:, 0:32, 16:24], 3, shf), _bc(X1[:, 0], 2, shf), ALU.mult, ["WFre", "x1", "Lf"], ["Lf"])
    tt("pool", tf[:], _bc(WFim[:, 0:32, 16:24], 3, shf), _bc(X2[:, 0], 2, shf), ALU.mult, ["WFim", "x2"], ["tf"])
    tt("dve", Lf[:], Lf[:], tf[:], ALU.add, ["Lf", "tf"], ["Lf"])
    Rf = fprod("Rf", RT1[:, 0:32, 0:8], RT2[:, 0:32, 0:8], Y1[:, 0], Y2[:, 0])
    Rb = fprod("Rb", RT1[:, 32:64, 8:16], RT2[:, 32:64, 8:16], Y1[:, 1], Y2[:, 1])
    tpf = sb("tpf", [128, 128], F32)
    tpb = sb("tpb", [128, 128], F32)
    for g in range(32):
        b1, b2 = C.nb(), C.nb()
        fl = lambda T: T[:, g].rearrange("n i h -> n (i h)")
        mm(C.bank(b1)[:, 0:128], fl(Lf), fl(Rf), True, True, ["Lf", "Rf"], [C.bk(b1)])
        mm(C.bank(b2)[:, 0:128], fl(Lb), fl(Rb), True, True, ["Lb", "Rb"], [C.bk(b2)])
        tt("dve", tpf[:], C.bank(b1)[:, 0:128], maskf[:], ALU.mult, [C.bk(b1), "maskf"], ["tpf"])
        tt("dve", tpb[:], C.bank(b2)[:, 0:128], maskb[:], ALU.mult, [C.bk(b2), "maskb"], ["tpb"])
        tt("dve", tpf[:], tpf[:], tpb[:], ALU.add, ["tpf", "tpb"], ["tpf"])
        stt(C.Toep[:, g, :], C.ident_f[:], drep[:, g:g + 1], tpf[:], ALU.mult, ALU.add, ["ident_f", "drep", "tpf"], ["Toep"])
    C.dump("Winb", C.Winb[:].rearrange("p g d r c -> p (g d r c)"), [128, 32 * 2 * 2 * 128], "Winb", BF16)
    C.dump("Toep", C.Toep[:].rearrange("p g c -> p (g c)"), [128, 4096], "Toep", BF16)
    C.st(C.Toepd[:, :, :], C.Toep[:], "SToep", ["Toep"])
    P.release(m0)
    P.barrier()
    C.CoutS = sb("CoutS", [128, 2, 16, 2, 128], BF16)
    ncs = 32
    ares = sb("are_s", [128, ncs], F32)
    aims = sb("aim_s", [128, ncs], F32)
    ldts = sb("ldt_s", [128, ncs], F32)
    ld(ares[:], D["are_s"].rearrange("n d g -> n (d g)"), ["Sare"])
    ld(aims[:], D["aim_s"].rearrange("n d g -> n (d g)"), ["Saim"])
    ld(ldts[:], D["ldt_s"].rearrange("n d g -> n (d g)"), ["Sldt"])
    CS1 = sb("cs1", [128, 2, 16, 16], F32)
    CS2 = sb("cs2", [128, 2, 16, 16], F32)
    ld(CS1[:], D["cs1"][:, :, :, :], ["cs1"])
    ld(CS2[:], D["cs2"][:, :, :, :], ["cs2"])
    SPre, SPim, _, _ = _powers(C, "S", ares[:], aims[:], ldts[:], ncs, (24, 42), 18)
    shc = [128, 16, 8, 16]
    cr = sb("cr", shc, F32)
    ci = sb("ci", shc, F32)
    tc_ = sb("tc", shc, F32)
    for d in range(2):
        e0 = 8 * d
        pre = SPre[:, 16 * d:16 * d + 16, e0:e0 + 8]
        pim = SPim[:, 16 * d:16 * d + 16, e0:e0 + 8]
        a1 = _bc(CS1[:, d], 2, shc)
        a2 = _bc(CS2[:, d], 2, shc)
        kk = ["Spwre", "Spwim", "cs1", "cs2"]
        tt("dve", cr[:], _bc(pre, 3, shc), a1, ALU.mult, kk, ["cr"])
        tt("pool", tc_[:], _bc(pim, 3, shc), a2, ALU.mult, kk, ["tc"])
        tt("dve", cr[:], cr[:], tc_[:], ALU.subtract, ["cr", "tc"], ["cr"])
        cp("dve", C.CoutS[:, d, :, 0, :], cr[:].rearrange("p c j h -> p c (j h)"), ["cr"], ["CoutS"])
        tt("dve", ci[:], _bc(pim, 3, shc), a1, ALU.mult, kk, ["ci"])
        tt("pool", tc_[:], _bc(pre, 3, shc), a2, ALU.mult, kk, ["tc"])
        tt("dve", ci[:], ci[:], tc_[:], ALU.add, ["ci", "tc"], ["ci"])
        ts("dve", ci[:], ci[:], -1.0, None, ALU.mult, None, ["ci"], ["ci"])
        cp("dve", C.CoutS[:, d, :, 1, :], ci[:].rearrange("p c j h -> p c (j h)"), ["ci"], ["CoutS"])
    C.dump("CoutS", C.CoutS[:].rearrange("p d c r k -> p (d c r k)"), [128, 2 * 16 * 2 * 128], "CoutS", BF16)
    cp("dve", C.Mre[:, :, 0], SPre[:, :, 17], ["Spwre"], ["Mre"])
    cp("dve", C.Mim[:, :, 0], SPim[:, :, 17], ["Spwim"], ["Mim"])
    q1 = sb("q1", [128, ncs], F32)
    q2 = sb("q2", [128, ncs], F32)
    for j in range(9):
        tt("dve", q1[:], C.Mre[:, :, j], C.Mre[:, :, j], ALU.mult, ["Mre"], ["q1"])
        tt("dve", q2[:], C.Mim[:, :, j], C.Mim[:, :, j], ALU.mult, ["Mim"], ["q2"])
        tt("dve", C.Mre[:, :, j + 1], q1[:], q2[:], ALU.subtract, ["q1", "q2", "Mre"], ["Mre"])
        tt("dve", q1[:], C.Mre[:, :, j], C.Mim[:, :, j], ALU.mult, ["Mre", "Mim"], ["q1"])
        ts("dve", C.Mim[:, :, j + 1], q1[:], 2.0, None, ALU.mult, None, ["q1", "Mim"], ["Mim"])
    ts("dve", C.Mimn[:], C.Mim[:], -1.0, None, ALU.mult, None, ["Mim"], ["Mimn"])
    C.dump("Mre", C.Mre[:].rearrange("p c j -> p (c j)"), [128, 320], "Mre")
    C.dump("Mim", C.Mim[:].rearrange("p c j -> p (c j)"), [128, 320], "Mim")
    C.st(C.Coutd[:, :, :, :, :], C.CoutS[:], "SCout", ["CoutS"])
    P.release(m0)
    P.barrier()


def build_pw(C):
    P, op, tt, ts, cp, ms = C.P, C.op, C.tt, C.ts, C.cp, C.ms
    sb = P.sb
    C.Pw = sb("Pw", [128, 32, 2, 128], F32)
    mpw = P.mark()
    ms("pool", C.Pw[:, 0:16, 0, 127:128], 1.0, ["Pw"])
    ms("pool", C.Pw[:, 0:16, 1, 127:128], 0.0, ["Pw"])
    ms("pool", C.Pw[:, 16:32, 0, 0:1], 1.0, ["Pw"])
    ms("pool", C.Pw[:, 16:32, 1, 0:1], 0.0, ["Pw"])
    u1 = sb("u1", [128, 16, 64], F32)
    u2 = sb("u2", [128, 16, 64], F32)
    for j in range(7):
        w = 1 << j
        for d in range(2):
            dc = slice(16 * d, 16 * d + 16)
            if d == 0:
                src = slice(128 - w, 128)
                dst = slice(128 - 2 * w, 128 - w)
            else:
                src = slice(0, w)
                dst = slice(w, 2 * w)
            shp = [128, 16, w]
            mre = _bc(C.Mre[:, dc, j], 2, shp)
            mim = _bc(C.Mim[:, dc, j], 2, shp)
            sre = C.Pw[:, dc, 0, src]
            sim = C.Pw[:, dc, 1, src]
            tt("dve", u1[:, :, 0:w], sre, mre, ALU.mult, ["Pw", "Mre"], ["u1"])
            tt("dve", u2[:, :, 0:w], sim, mim, ALU.mult, ["Pw", "Mim"], ["u2"])
            tt("dve", u1[:, :, 0:w], u1[:, :, 0:w], u2[:, :, 0:w], ALU.subtract, ["u1", "u2"], ["u1"])
            tt("dve", u2[:, :, 0:w], sre, mim, ALU.mult, ["Pw", "Mim"], ["u2"])
            cp("dve", C.Pw[:, dc, 0, dst], u1[:, :, 0:w], ["u1", "Pw"], ["Pw"])
            tt("dve", u1[:, :, 0:w], sim, mre, ALU.mult, ["Pw", "Mre"], ["u1"])
            tt("dve", C.Pw[:, dc, 1, dst], u1[:, :, 0:w], u2[:, :, 0:w], ALU.add, ["u1", "u2", "Pw"], ["Pw"])
    C.dump("Pw", C.Pw[:].rearrange("p c r k -> p (c r k)"), [128, 32 * 2 * 128], "Pw")
    P.release(mpw)
    P.barrier()


def load_win_cols(C, winb, colmap, tagp):
    P, D = C.P, C.D
    m = P.mark()
    st = [P.sb(tagp + "wst0", [128, 1952], F32), P.sb(tagp + "wst1", [128, 1952], F32)]
    for k in range(8):
        s = st[k % 2]
        key = tagp + f"wst{k % 2}"
        C.ld(s[:], D["w_in_t"][:, k, :], [key])
        for n_, (d0, s0, n, sign) in enumerate(colmap):
            sc = C.pre_w[:, k:k + 1] if sign > 0 else C.npre_w[:, k:k + 1]
            C.act(winb[:, k, d0:d0 + n], s[:, s0:s0 + n], AF.Identity, [key, "pre_w"], [tagp + "winb"], scale=sc)
    P.release(m)


def front_block(C, src, npart, xnT, c0, tagp, bi):
    P = C.P
    xb = C.xblk[bi % 2]
    kx = f"xblk{bi % 2}"
    C.ld(xb[0:npart, :], src, [kx])
    ssq = C.ssq[:, bi % 4:bi % 4 + 1]
    ks = f"ssq{bi % 4}"
    C.stt(C.junk[0:npart, :], xb[0:npart, :], 1.0, xb[0:npart, :], ALU.mult, ALU.mult, [kx], ["junk", ks], accum=ssq[0:npart])
    C.act(ssq[0:npart], ssq[0:npart], AF.Sqrt, [ks], [ks], scale=1.0 / 1024.0, bias=EPS)
    C.op("dve", lambda e: e.reciprocal(ssq[0:npart], ssq[0:npart]), [ks], [ks])
    xn = C.xn[bi % 2]
    kn = f"xn{bi % 2}"
    C.act(xn[0:npart, :], xb[0:npart, :], AF.Identity, [kx, ks], [kn], scale=ssq[0:npart])
    b = C.nb()
    for k in range(8):
        C.tr(C.bankb(b)[:, 128 * k:128 * k + npart], xn[0:npart, 128 * k:128 * k + 128], C.ident_b[0:npart, 0:npart],
             [kn, "ident_b"], [C.bk(b)])
    src3 = C.bankb(b).rearrange("p (k c) -> p k c", k=8)[:, :, 0:npart]
    C.cp("act" if bi % 2 else "dve", xnT[:, :, c0:c0 + npart], src3, [C.bk(b)], [tagp])


def phase_global(C):
    P, D, nc = C.P, C.D, C.nc
    sb, op, mm, tr, act, tt, ts, stt, cp, ms, ld = P.sb, C.op, C.mm, C.tr, C.act, C.tt, C.ts, C.stt, C.cp, C.ms, C.ld
    C.Kd = nc.dram_tensor("Kd", [8, 64, 16400], BF16).ap()
    C.Krd = nc.dram_tensor("Krd", [32, 16400], BF16).ap()
    C.Vd = nc.dram_tensor("Vd", [8, 128, 129, 128], BF16).ap()
    ld(C.pre_w[:], D["pre_w"][:, :], ["pre_w"])
    ts("dve", C.npre_w[:], C.pre_w[:], -1.0, None, ALU.mult, None, ["pre_w"], ["pre_w"])
    ms("pool", C.accs[:], 0.0, ["accs"])
    P.barrier()
    if "pg" in C.skip:
        return
    mG = P.mark()
    build_pw(C)
    winb = sb("winb_g", [128, 8, 832], BF16)
    ms("pool", winb[:, :, 128:192], 0.0, ["gwinb"])
    ms("pool", winb[:, :, 224:288], 0.0, ["gwinb"])
    load_win_cols(C, winb, [(0, 256, 128, 1), (192, 384, 32, 1), (288, 400, 16, -1), (304, 384, 16, 1), (320, 928, 512, 1)], "g")
    kvnw = sb("kvnw", [128, 1], F32)
    wkvb = sb("wkvb", [128, 1024], BF16)
    wkvv = sb("wkvv", [128, 512], BF16)
    mw = P.mark()
    wkvf = sb("wkvf", [128, 1024], F32)
    ld(wkvf[:], D["wkv"][:, :], ["wkvf"])
    ld(kvnw[:], D["kvn_w"][:, :], ["kvnw"])
    act(wkvb[:], wkvf[:], AF.Identity, ["wkvf", "kvnw"], ["wkvb"], scale=kvnw[:, 0:1])
    cp("dve", wkvv[:].rearrange("p (h e) -> p h e", h=8), wkvb[:].rearrange("p (h e) -> p h e", h=8)[:, :, 64:128], ["wkvb"], ["wkvv"])
    P.release(mw)
    P.barrier()
    C.xblk = [sb(f"xblk{i}", [128, 1024], F32) for i in range(2)]
    C.xn = [sb(f"xn{i}", [128, 1024], BF16) for i in range(2)]
    C.junk = sb("junk", [128, 1024], BF16)
    xnT = [sb(f"xnT{i}", [128, 8, 512], BF16) for i in range(2)]
    sq_f = sb("sq_f", [128, 512], F32)
    rs = sb("rs", [128, 512], F32)
    kvn = [sb(f"kvn{i}", [128, 512], BF16) for i in range(2)]
    cost = [sb(f"cost{i}", [128, 512], F32) for i in range(1)] * 2
    sint = [sb(f"sint{i}", [128, 512], F32) for i in range(1)] * 2
    tmpa = sb("tmpa", [128, 512], F32)
    tmpb = sb("tmpb", [128, 512], F32)
    Krst = [sb(f"Krst{i}", [128, 512], BF16) for i in range(2)]
    Kst = [sb(f"Kst{i}", [64, 8, 512], BF16) for i in range(1)] * 2
    Vst = [sb(f"Vst{i}", [128, 8, 8, 128], BF16) for i in range(1)] * 2
    ms("pool", Vst[0][:, :, :, 64:128], 1.0, ["Vst0"])
    u_tm = [sb(f"u_tm{i}", [128, 32, 8, 16], BF16) for i in range(1)] * 2
    U8t = [sb(f"U8t{i}", [128, 32, 128], BF16) for i in range(1)] * 2
    ejunk = sb("ejunk", [128, 8, 128], F32)
    C.ej_n = 0
    xall4 = D["x_all"].rearrange("(t c i) d -> t c i d", c=128, i=8)
    meta3 = D["meta"].rearrange("(c i) d -> c i d", i=8)
    Kd_r = C.Kd.rearrange("h r c -> r h c")
    Vd_p = C.Vd.rearrange("h p b e -> p h b e")
    bi = 0
    hi_ = 0
    for T in list(range(16)) + [16]:
        meta = T == 16
        npart = 2 if meta else 128
        halves = [list(range(8))] if meta else [[0, 1, 2, 3], [4, 5, 6, 7]]
        ut = u_tm[T % 2]
        kut = "u_tm0"
        vs = Vst[T % 2]
        kvs = "Vst0"
        for hh, blocks in enumerate(halves):
            ncols = len(blocks) * npart
            col0 = 16384 if meta else 1024 * T + 512 * hh
            xT = xnT[hi_ % 2]
            kxT = f"xnT{hi_ % 2}"
            ctb, stb = cost[hi_ % 2], sint[hi_ % 2]
            kct, kst_ = "cost0", "sint0"
            ld(ctb[64:96, 0:ncols], D["cosk"][:, col0:col0 + ncols], [kct])
            ld(stb[64:96, 0:ncols], D["sink"][:, col0:col0 + ncols], [kst_])
            for n_, i in enumerate(blocks):
                src = meta3[:, i, :] if meta else xall4[T, :, i, :]
                front_block(C, src, npart, xT, n_ * npart, kxT, bi)
                bi += 1
            b_kv, b_kr, b_ks = C.nb(), C.nb(), C.nb()
            for (bb, c0_, M) in ((b_kv, 0, 128), (b_kr, 128, 96), (b_ks, 224, 96)):
                for k in range(8):
                    mm(C.bank(bb)[0:M, 0:ncols], winb[:, k, c0_:c0_ + M], xT[:, k, 0:ncols], k == 0, k == 7,
                       ["gwinb", kxT], [C.bk(bb)])
            act(sq_f[:, 0:ncols], C.bank(b_kv)[:, 0:ncols], AF.Square, [C.bk(b_kv)], ["sq_f"])
            b_ss = C.nb()
            mm(C.bank(b_ss)[:, 0:ncols], C.ones_f[:], sq_f[:, 0:ncols], True, True, ["ones_f", "sq_f"], [C.bk(b_ss)])
            act(rs[:, 0:ncols], C.bank(b_ss)[:, 0:ncols], AF.Sqrt, [C.bk(b_ss)], ["rs"], scale=1.0 / 128.0, bias=EPS)
            op("dve", lambda e, a=rs[:, 0:ncols]: e.reciprocal(a, a), ["rs"], ["rs"])
            kv_ = kvn[hi_ % 2]
            kkv = f"kvn{hi_ % 2}"
            tt("dve", kv_[:, 0:ncols], C.bank(b_kv)[:, 0:ncols], rs[:, 0:ncols], ALU.mult, [C.bk(b_kv), "rs"], [kkv])
            tt("dve", tmpa[64:96, 0:ncols], C.bank(b_kr)[64:96, 0:ncols], ctb[64:96, 0:ncols], ALU.mult, [C.bk(b_kr), kct], ["tmpa"])
            tt("dve", tmpb[64:96, 0:ncols], C.bank(b_ks)[64:96, 0:ncols], stb[64:96, 0:ncols], ALU.mult, [C.bk(b_ks), kst_], ["tmpb"])
            krs = Krst[hi_ % 2]
            kkrs = f"Krst{hi_ % 2}"
            tt("pool", krs[64:96, 0:ncols], tmpa[64:96, 0:ncols], tmpb[64:96, 0:ncols], ALU.add, ["tmpa", "tmpb"], [kkrs])
            C.st(C.Krd[:, col0:col0 + ncols], krs[64:96, 0:ncols], "S" + kkrs, [kkrs])
            kst = Kst[hi_ % 2]
            kkst = "Kst0"
            for h in range(8):
                bb = C.nb()
                mm(C.bank(bb)[0:64, 0:ncols], wkvb[:, 128 * h:128 * h + 64], kv_[:, 0:ncols], True, True, ["wkvb", kkv], [C.bk(bb)])
                cp("act" if h % 2 else "dve", kst[:, h, 0:ncols], C.bank(bb)[0:64, 0:ncols], [C.bk(bb)], [kkst])
            C.st(Kd_r[:, :, col0:col0 + ncols], kst[:, :, 0:ncols], "S" + kkst, [kkst])
            if meta:
                bb = C.nb()
                mm(C.bank(bb)[0:16, 0:512], kv_[:, 0:16], wkvv[:], True, True, [kkv, "wkvv"], [C.bk(bb)])
                cp("act", vs[0:16, :, 0, 0:64], C.bank(bb)[0:16, 0:512].rearrange("p (h e) -> p h e", h=8), [C.bk(bb)], [kvs])
                C.st(Vd_p[:, :, 128, :], vs[:, :, 0, :], "S" + kvs, [kvs])
            else:
                for n_, i in enumerate(blocks):
                    bb = C.nb()
                    mm(C.bank(bb)[:, 0:512], kv_[:, 128 * n_:128 * n_ + 128], wkvv[:], True, True, [kkv, "wkvv"], [C.bk(bb)])
                    cp("act" if i % 2 else "dve", vs[:, :, i, 0:64], C.bank(bb)[:, 0:512].rearrange("p (h e) -> p h e", h=8), [C.bk(bb)], [kvs])
            for n_, i in enumerate(blocks):
                bb = C.nb()
                for k in range(8):
                    mm(C.bank(bb)[0:npart, 0:512], xT[:, k, n_ * npart:(n_ + 1) * npart], winb[:, k, 320:832], k == 0, k == 7,
                       [kxT, "gwinb"], [C.bk(bb)])
                cp("dve" if i % 2 else "act", ut[0:npart, :, i, :], C.bank(bb)[0:npart, 0:512].rearrange("p (g h) -> p g h", g=32), [C.bk(bb)], [kut])
            hi_ += 1
        if not meta:
            C.st(Vd_p[:, :, 8 * T:8 * T + 8, :], vs[:, :, :, :], "S" + kvs, [kvs])
        u8 = U8t[T % 2]
        ku8 = "U8t0"
        for gq in range(4):
            bb = C.nb()
            for g8 in range(8):
                g = 8 * gq + g8
                tr(C.bankb(bb)[:, 128 * g8:128 * g8 + npart], ut[0:npart, g].rearrange("p i h -> p (i h)"), C.ident_b[0:npart, 0:npart],
                   [kut, "ident_b"], [C.bk(bb)])
            cp("act" if gq % 2 else "dve", u8[:, 8 * gq:8 * gq + 8, 0:npart], C.bankb(bb).rearrange("p (g c) -> p g c", g=8)[:, :, 0:npart], [C.bk(bb)], [ku8])
        if T == 0:
            C.dump("u_tm", ut[:].rearrange("p g i h -> p (g i h)"), [128, 4096], kut, BF16)
            C.dump("U8t", u8[:].rearrange("p g c -> p (g c)"), [128, 4096], ku8, BF16)
        for ct in range(16):
            bb = C.nb()
            for d in range(2):
                if meta and d == 1:
                    continue
                for ri in range(2):
                    r = 2 * d + ri
                    for g2 in range(2):
                        g = 2 * ct + g2
                        mm(C.bank(bb)[:, 128 * r:128 * r + npart], C.Winb[:, g, d, ri, :], u8[:, g, 0:npart], g2 == 0, g2 == 1,
                           ["Winb", ku8], [C.bk(bb)])
            if T == 0 and ct in (0, 15) and ("S8_%d" % ct) in C.dbg:
                s8c = sb("s8c%d" % ct, [128, 512], F32)
                cp("dve", s8c[:], C.bank(bb)[:, 0:512], [C.bk(bb)], ["s8c%d" % ct])
                C.dump("S8_%d" % ct, s8c[:], [128, 512], "s8c%d" % ct)
            for d in range(2):
                if meta and d == 1:
                    continue
                cs = slice(126, 128) if meta else slice(0, 128)
                S_re = C.bank(bb)[:, 128 * (2 * d):128 * (2 * d) + npart]
                S_im = C.bank(bb)[:, 128 * (2 * d + 1):128 * (2 * d + 1) + npart]
                p_re = C.Pw[:, 16 * d + ct, 0, cs]
                p_im = C.Pw[:, 16 * d + ct, 1, cs]
                for n4, (a_, b_) in enumerate(((S_re, p_re), (S_im, p_im), (S_re, p_im), (S_im, p_re))):
                    ej = C.ej_n % 8
                    C.ej_n += 1
                    stt(ejunk[:, ej, 0:npart], a_, 1.0, b_, ALU.mult, ALU.mult, [C.bk(bb), "Pw"], [f"ejunk{ej}", f"accs_{T}_{ct}_{d}_{n4}"],
                        accum=C.accs[:, T, ct, d, n4:n4 + 1])
    C.dump("accs", C.accs[:].rearrange("p t c d k -> p (t c d k)"), [128, 17 * 16 * 2 * 4], "accs")
    P.barrier()
    E = sb("Echain", [128, 17, 16, 2, 2], F32)
    tt("dve", E[:, :, :, :, 0], C.accs[:, :, :, :, 0], C.accs[:, :, :, :, 1], ALU.subtract, ["accs"], ["E"])
    tt("dve", E[:, :, :, :, 1], C.accs[:, :, :, :, 2], C.accs[:, :, :, :, 3], ALU.add, ["accs", "E"], ["E"])
    cf = sb("cf", [128, 2, 16], F32)
    cb_ = sb("cb", [128, 2, 16], F32)
    cn = sb("cn", [128, 2, 16], F32)
    t_a = sb("t_a", [128, 16], F32)
    CF = sb("CF", [128, 4, 2, 16], F32)
    CB = sb("CB", [128, 4, 2, 16], F32)
    def cstep(cur, d, Et):
        mre, mim = C.Mre[:, 16 * d:16 * d + 16, 7], C.Mim[:, 16 * d:16 * d + 16, 7]
        kk = ["cur", "Mre", "Mim", "E", "cn", "t_a"]
        tt("dve", cn[:, 0], cur[:, 0], mre, ALU.mult, kk, ["cn"])
        tt("dve", t_a[:], cur[:, 1], mim, ALU.mult, kk, ["t_a"])
        tt("dve", cn[:, 0], cn[:, 0], t_a[:], ALU.subtract, kk, ["cn"])
        tt("dve", cn[:, 0], cn[:, 0], E[:, Et, :, d, 0], ALU.add, kk, ["cn"])
        tt("dve", cn[:, 1], cur[:, 0], mim, ALU.mult, kk, ["cn"])
        tt("dve", t_a[:], cur[:, 1], mre, ALU.mult, kk, ["t_a"])
        tt("dve", cn[:, 1], cn[:, 1], t_a[:], ALU.add, kk, ["cn"])
        tt("dve", cn[:, 1], cn[:, 1], E[:, Et, :, d, 1], ALU.add, kk, ["cn"])
        cp("dve", cur[:], cn[:], ["cn", "cur"], ["cur"])
    cp("dve", cf[:, 0], E[:, 16, :, 0, 0], ["E"], ["cur"])
    cp("dve", cf[:, 1], E[:, 16, :, 0, 1], ["E", "cur"], ["cur"])
    for T in range(16):
        if T % 4 == 0:
            cp("dve", CF[:, T // 4], cf[:], ["cur"], ["CF"])
        cstep(cf, 0, T)
    ms("dve", cb_[:], 0.0, ["cur"])
    for T in range(15, -1, -1):
        if T % 4 == 3:
            cp("dve", CB[:, T // 4], cb_[:], ["cur"], ["CB"])
        cstep(cb_, 1, T)
    for d, CX in ((0, CF), (1, CB)):
        ts("dve", C.Xin[:, d], CX[:, 0], C.selq[:, 0:1], None, ALU.mult, None, ["CF", "CB", "selq"], ["Xin"])
        for r in range(1, 4):
            stt(C.Xin[:, d], CX[:, r], C.selq[:, r:r + 1], C.Xin[:, d], ALU.mult, ALU.add, ["CF", "CB", "selq", "Xin"], ["Xin"])
    for d in range(2):
        mre, mim = C.Mre[:, 16 * d:16 * d + 16, 0], C.Mim[:, 16 * d:16 * d + 16, 0]
        kk = ["Xin", "Mre", "Mim", "AXin", "t_a"]
        tt("dve", C.AXin[:, d, 0], C.Xin[:, d, 0], mre, ALU.mult, kk, ["AXin"])
        tt("dve", t_a[:], C.Xin[:, d, 1], mim, ALU.mult, kk, ["t_a"])
        tt("dve", C.AXin[:, d, 0], C.AXin[:, d, 0], t_a[:], ALU.subtract, kk, ["AXin"])
        tt("dve", C.AXin[:, d, 1], C.Xin[:, d, 0], mim, ALU.mult, kk, ["AXin"])
        tt("dve", t_a[:], C.Xin[:, d, 1], mre, ALU.mult, kk, ["t_a"])
        tt("dve", C.AXin[:, d, 1], C.AXin[:, d, 1], t_a[:], ALU.add, kk, ["AXin"])
    C.dump("Xin", C.Xin[:].rearrange("p d r c -> p (d r c)"), [128, 64], "Xin")
    C.dump("Kd", C.Kd.rearrange("h r c -> (h r) c"), [512, 16400], "Kd", BF16)
    C.dump("Krd", C.Krd, [32, 16400], "Krd", BF16)
    C.dump("Vd", C.Vd.rearrange("h p b e -> (h p) (b e)"), [1024, 129 * 128], "Vd", BF16)
    P.release(mG)
    P.barrier()


def phase_local(C):
    P, D, nc = C.P, C.D, C.nc
    sb, op, mm, tr, act, tt, ts, stt, cp, ms, ld = P.sb, C.op, C.mm, C.tr, C.act, C.tt, C.ts, C.stt, C.cp, C.ms, C.ld
    C.Qd = nc.dram_tensor("Qd", [8, 96, 4096], BF16).ap()
    C.Gad = nc.dram_tensor("Gad", [512, 4096], BF16).ap()
    C.Gsd = nc.dram_tensor("Gsd", [512, 4096], BF16).ap()
    C.mL = P.mark()
    C.U8loc = sb("U8loc", [128, 32, 512], BF16)
    mP = P.mark()
    winb = sb("winb_l", [128, 8, 1792], BF16)
    load_win_cols(C, winb, [(0, 0, 256, 1), (256, 416, 512, 1), (768, 928, 512, 1), (1280, 1440, 512, 1)], "l")
    wqb = sb("wqb", [128, 2, 768], BF16)
    wqsw = sb("wqsw", [128, 2, 768], BF16)
    qnw = sb("qnw", [128, 2], F32)
    mw = P.mark()
    wqf = sb("wqf", [128, 2, 768], F32)
    ld(wqf[:], D["wq_t"][:, :, :], ["wqf"])
    ld(qnw[:], D["qn_w"][:, :], ["qnw"])
    for k in range(2):
        act(wqb[:, k, :], wqf[:, k, :], AF.Identity, ["wqf", "qnw"], ["wqb"], scale=qnw[:, k:k + 1])
    ms("pool", wqsw[:], 0.0, ["wqsw"])
    wq4 = wqb[:].rearrange("p k (h e) -> p k h e", e=96)
    ws4 = wqsw[:].rearrange("p k (h e) -> p k h e", e=96)
    for k in range(2):
        ts("dve", ws4[:, k, :, 64:80], wq4[:, k, :, 80:96], -1.0, None, ALU.mult, None, ["wqb", "wqsw"], ["wqsw"])
        cp("dve", ws4[:, k, :, 80:96], wq4[:, k, :, 64:80], ["wqb", "wqsw"], ["wqsw"])
    P.release(mw)
    C.xblk = [sb(f"lxblk{i}", [128, 1024], F32) for i in range(2)]
    C.xn = [sb(f"lxn{i}", [128, 1024], BF16) for i in range(2)]
    C.junk = sb("ljunk", [128, 1024], BF16)
    xT = sb("lxnT", [128, 8, 512], BF16)
    sqq = sb("sqq", [128, 2, 512], F32)
    rsq = sb("rsq", [128, 512], F32)
    qlb = sb("qlb", [128, 2, 512], BF16)
    c96t = sb("c96t", [128, 512], F32)
    s96t = sb("s96t", [128, 512], F32)
    t1 = sb("lt1", [128, 512], F32)
    t2 = sb("lt2", [128, 512], F32)
    Qst = sb("Qst", [96, 8, 512], BF16)
    gst = sb("gst", [128, 4, 512], BF16)
    u_tm = sb("u_tmL", [128, 4, 32, 8, 16], BF16)
    xloc4 = D["x_loc"].rearrange("(t c i) d -> t c i d", c=128, i=8)
    Qd_r = C.Qd.rearrange("h r c -> r h c")
    bi = 0
    if C.stop == "pl0":
        return
    for i in range(8):
        if C.stop == "pl1" and i == 1:
            return
        c0 = 512 * i
        ld(c96t[0:96, :], D["c96"][:, c0:c0 + 512], ["c96t"])
        ld(s96t[0:96, :], D["s96"][:, c0:c0 + 512], ["s96t"])
        for T in range(4):
            front_block(C, xloc4[T, :, i, :], 128, xT, 128 * T, "lxnT", bi)
            bi += 1
        if C.stop == "plf":
            return
        bq = [C.nb(), C.nb()]
        for m in range(2):
            for k in range(8):
                mm(C.bank(bq[m])[:, :], winb[:, k, 128 * m:128 * m + 128], xT[:, k, :], k == 0, k == 7, ["lwinb", "lxnT"], [C.bk(bq[m])])
            act(sqq[:, m, :], C.bank(bq[m])[:, :], AF.Square, [C.bk(bq[m])], ["sqq"])
            cp("act", qlb[:, m, :], C.bank(bq[m])[:, :], [C.bk(bq[m])], ["qlb"])
        if C.stop == "pln":
            return
        bs = C.nb()
        for m in range(2):
            mm(C.bank(bs)[:, :], C.ones_f[:], sqq[:, m, :], m == 0, m == 1, ["ones_f", "sqq"], [C.bk(bs)])
        if C.stop == "plo":
            return
        act(rsq[:], C.bank(bs)[:, :], AF.Sqrt, [C.bk(bs)], ["rsq"], scale=1.0 / 256.0, bias=EPS)
        op("dve", lambda e: e.reciprocal(rsq[:], rsq[:]), ["rsq"], ["rsq"])
        if C.stop == "plm":
            return
        for h in range(8):
            br, bw = C.nb(), C.nb()
            for k in range(2):
                mm(C.bank(br)[0:96, :], wqb[:, k, 96 * h:96 * h + 96], qlb[:, k, :], k == 0, k == 1, ["wqb", "qlb"], [C.bk(br)])
            for k in range(2):
                mm(C.bank(bw)[0:96, :], wqsw[:, k, 96 * h:96 * h + 96], qlb[:, k, :], k == 0, k == 1, ["wqsw", "qlb"], [C.bk(bw)])
            tt("dve", t1[0:96, :], C.bank(br)[0:96, :], c96t[0:96, :], ALU.mult, [C.bk(br), "c96t"], ["lt1"])
            tt("dve", t2[0:96, :], C.bank(bw)[0:96, :], s96t[0:96, :], ALU.mult, [C.bk(bw), "s96t"], ["lt2"])
            tt("dve", t1[0:96, :], t1[0:96, :], t2[0:96, :], ALU.add, ["lt1", "lt2"], ["lt1"])
            tt("dve", Qst[:, h, :], t1[0:96, :], rsq[0:96, :], ALU.mult, ["lt1", "rsq"], ["Qst"])
        if C.stop == "plq":
            return
        C.st(Qd_r[:, :, c0:c0 + 512], Qst[:, :, :], "SQst", ["Qst"])
        if C.stop == "plQ":
            return
        for (woff, dst, nm) in ((256, C.Gad, "a"), (1280, C.Gsd, "s")):
            for m in range(4):
                bb = C.nb()
                for k in range(8):
                    mm(C.bank(bb)[:, :], winb[:, k, woff + 128 * m:woff + 128 * m + 128], xT[:, k, :], k == 0, k == 7, ["lwinb", "lxnT"], [C.bk(bb)])
                act(gst[:, m, :], C.bank(bb)[:, :], AF.Silu, [C.bk(bb)], ["gst"])
            C.st(dst.rearrange("(m p) c -> p m c", p=128)[:, :, c0:c0 + 512], gst[:, :, :], "Sgst", ["gst"])
        for T in range(4):
            bb = C.nb()
            for k in range(8):
                mm(C.bank(bb)[:, :], xT[:, k, 128 * T:128 * T + 128], winb[:, k, 768:1280], k == 0, k == 7, ["lxnT", "lwinb"], [C.bk(bb)])
            cp("dve" if T % 2 else "act", u_tm[:, T, :, i, :], C.bank(bb)[:, :].rearrange("p (g h) -> p g h", g=32), [C.bk(bb)], ["u_tmL"])
    for T in range(4):
        for gq in range(4):
            bb = C.nb()
            for g8 in range(8):
                g = 8 * gq + g8
                tr(C.bankb(bb)[:, 128 * g8:128 * g8 + 128], u_tm[:, T, g].rearrange("p i h -> p (i h)"), C.ident_b[:], ["u_tmL", "ident_b"], [C.bk(bb)])
            cp("act" if gq % 2 else "dve", C.U8loc[:, 8 * gq:8 * gq + 8, 128 * T:128 * T + 128], C.bankb(bb).rearrange("p (g c) -> p g c", g=8), [C.bk(bb)], ["U8loc"])
    C.dump("U8loc", C.U8loc[:].rearrange("p g c -> p (g c)"), [128, 32 * 512], "U8loc", BF16)
    C.dump("Qd", C.Qd.rearrange("h r c -> (h r) c"), [768, 4096], "SQst", BF16)
    P.release(mP)


def phase_ssm(C):
    P, D, nc = C.P, C.D, C.nc
    sb, op, mm, tr, act, tt, ts, stt, cp, ms, ld = P.sb, C.op, C.mm, C.tr, C.act, C.tt, C.ts, C.stt, C.cp, C.ms, C.ld
    C.Y8d = nc.dram_tensor("Y8d", [32, 128, 512], BF16).ap()
    Toep = sb("ToepL", [128, 32, 128], BF16)
    Cout = sb("CoutL", [128, 2, 16, 2, 128], BF16)
    ld(Toep[:], C.Toepd[:, :, :], ["ToepL"])
    ld(Cout[:], C.Coutd[:, :, :, :, :], ["CoutL"])
    HA = [[sb(f"HA{d}{r}", [128, 1024], F32) for r in range(2)] for d in range(2)]
    HB = [[sb(f"HB{d}{r}", [128, 1024], F32) for r in range(2)] for d in range(2)]
    for d in range(2):
        for r in range(2):
            ms("pool", HA[d][r][:], 0.0, [f"HA{d}"])
            ms("pool", HB[d][r][:], 0.0, [f"HB{d}"])
    Xp = [[sb(f"Xp{d}{r}", [128, 512], BF16) for r in range(2)] for d in range(2)]
    Yst = [sb(f"Yst{i}", [128, 512], BF16) for i in range(2)]
    for ct in range(16):
        banks = [C.nb() for _ in range(4)]
        for d in range(2):
            for ri in range(2):
                bb = banks[2 * d + ri]
                for g2 in range(2):
                    g = 2 * ct + g2
                    mm(C.bank(bb)[:, :], C.Winb[:, g, d, ri, :], C.U8loc[:, g, :], g2 == 0, g2 == 1, ["Winb", "U8loc"], [C.bk(bb)])
                act(HA[d][ri][:, 256:768], C.bank(bb)[:, :], AF.Copy, [C.bk(bb)], [f"HA{d}"])
        st_ = {}
        for d in range(2):
            col = 256 if d == 0 else 767
            for ri in range(2):
                tt("dve", HA[d][ri][:, col:col + 1], HA[d][ri][:, col:col + 1], C.AXin[:, d, ri, ct:ct + 1], ALU.add, [f"HA{d}", "AXin"], [f"HA{d}"])
            st_[d] = [HA[d], HB[d], f"HA{d}", f"HB{d}"]
        for j in range(9):
            s_ = 1 << j
            for d in range(2):
                src, dst, ks, kd = st_[d]
                lo = 256 - s_ if d == 0 else 256 + s_
                sh_re, sh_im = src[0][:, lo:lo + 512], src[1][:, lo:lo + 512]
                mre = C.Mre[:, 16 * d + ct, j:j + 1]
                mim = C.Mim[:, 16 * d + ct, j:j + 1]
                mimn = C.Mimn[:, 16 * d + ct, j:j + 1]
                kk = [ks, kd, "Mre", "Mim", "Mimn"]
                stt(dst[0][:, 256:768], sh_re, mre, src[0][:, 256:768], ALU.mult, ALU.add, kk, [kd + "r"])
                stt(dst[1][:, 256:768], sh_im, mre, src[1][:, 256:768], ALU.mult, ALU.add, kk, [kd + "i"])
                stt(dst[0][:, 256:768], sh_im, mimn, dst[0][:, 256:768], ALU.mult, ALU.add, kk + [kd + "r"], [kd + "r"])
                stt(dst[1][:, 256:768], sh_re, mim, dst[1][:, 256:768], ALU.mult, ALU.add, kk + [kd + "i"], [kd + "i", kd])
                st_[d] = [dst, src, kd, ks]
        for d in range(2):
            src, dst, ks, kd = st_[d]
            for ri in range(2):
                if d == 0:
                    cp("pool", Xp[d][ri][:, 1:512], src[ri][:, 256:767], [ks], [f"Xp{d}"])
                    cp("pool", Xp[d][ri][:, 0:1], C.Xin[:, d, ri, ct:ct + 1], ["Xin"], [f"Xp{d}"])
                else:
                    cp("pool", Xp[d][ri][:, 0:511], src[ri][:, 257:768], [ks], [f"Xp{d}"])
                    cp("pool", Xp[d][ri][:, 511:512], C.Xin[:, d, ri, ct:ct + 1], ["Xin"], [f"Xp{d}"])
        for g2 in range(2):
            g = 2 * ct + g2
            bb = C.nb()
            rows = slice(64 * g2, 64 * g2 + 64)
            mm(C.bank(bb)[:, :], Toep[:, g, :], C.U8loc[:, g, :], True, False, ["ToepL", "U8loc"], [C.bk(bb)])
            n = 0
            for d in range(2):
                for ri in range(2):
                    n += 1
                    mm(C.bank(bb)[:, :], Cout[rows, d, ct, ri, :], Xp[d][ri][rows, :], False, n == 4, ["CoutL", f"Xp{d}"], [C.bk(bb)])
            ys = Yst[g % 2]
            act(ys[:], C.bank(bb)[:, :], AF.Copy, [C.bk(bb)], [f"Yst{g % 2}"])
            C.st(C.Y8d[g, :, :], ys[:], f"SYst{g % 2}", [f"Yst{g % 2}"])
    C.dump("Y8d", C.Y8d.rearrange("g p c -> (g p) c"), [4096, 512], "SYst1", BF16)
    P.release(C.mBase)


def phase_attn(C):
    P, D, nc = C.P, C.D, C.nc
    sb, op, mm, tr, act, tt, ts, stt, cp, ms, ld = P.sb, C.op, C.mm, C.tr, C.act, C.tt, C.ts, C.stt, C.cp, C.ms, C.ld
    C.Yad = nc.dram_tensor("Yad", [512, 4096], BF16).ap()
    mA = P.mark()
    Kb = [sb(f"Kb{i}", [128, 16400], BF16) for i in range(2)]
    Vb = [sb(f"Vb{i}", [128, 129, 128], BF16) for i in range(2)]
    Qb = [sb(f"Qb{i}", [128, 4096], BF16) for i in range(2)]
    Pt = [sb(f"Pt{i}", [128, 1024], BF16) for i in range(3)]
    Osb = sb("Osb", [128, 512], F32)
    rd = sb("rd", [64, 512], F32)
    ya = sb("ya", [64, 512], F32)
    gt = [sb(f"gt{i}", [64, 512], BF16) for i in range(2)]
    yast = [sb(f"yast{i}", [64, 512], BF16) for i in range(2)]
    for i in range(2):
        ld(Kb[i][64:96, :], C.Krd[:, :], [f"Kr{i}"])
    S_pairs = [(0, 1), (2, 3), (4, 5)]
    npair = 0
    nfin = 0
    def load_head(h):
        hb = h % 2
        kK, kV, kQ = f"Kb{hb}", f"Vb{hb}", f"Qb{hb}"
        for part in range(4):
            cs = slice(4100 * part, 4100 * (part + 1))
            ld(Kb[hb][0:64, cs], C.Kd[h, :, cs], [kK + f"_{part}"])
        ld(Vb[hb][:, 0:64, :], C.Vd[h, :, 0:64, :], [kV])
        ld(Vb[hb][:, 64:129, :], C.Vd[h, :, 64:129, :], [kV + "m"])
        ld(Qb[hb][0:96, :], C.Qd[h, :, :], [kQ])

    load_head(0)
    for h in range(8):
        hb = h % 2
        kK, kV, kQ = f"Kb{hb}", f"Vb{hb}", f"Qb{hb}"
        if h + 1 < 8:
            load_head(h + 1)
        kKall = [kK + f"_{p}" for p in range(4)] + [f"Kr{hb}"]
        kVall = [kV, kV + "m"]
        for qt in range(8):
            bo = 6 + (qt % 2)
            qs = slice(512 * qt, 512 * qt + 512)
            ld(gt[nfin % 2][:, :], C.Gad[64 * h:64 * h + 64, qs], [f"gt{nfin % 2}"])

            def s_pair(n):
                a0, a1 = S_pairs[(npair + n) % 3]
                for (a, kb) in ((a0, 2 * n), (a1, 2 * n + 1)):
                    mm(C.bank(a)[:, :], Kb[hb][0:96, 128 * kb:128 * kb + 128], Qb[hb][0:96, qs], True, True, kKall + [kQ], [C.bk(a)])
                pt = Pt[(npair + n) % 3]
                act(pt[:, :], C.psall[:, 512 * a0:512 * a0 + 1024], AF.Exp, [C.bk(a0), C.bk(a1)], [f"Pt{(npair + n) % 3}"])

            def pv_pair(n):
                pt = Pt[(npair + n) % 3]
                for half, kb in ((0, 2 * n), (1, 2 * n + 1)):
                    mm(C.bank(bo)[:, :], Vb[hb][:, kb, :], pt[:, 512 * half:512 * half + 512], kb == 0, False,
                       kVall + [f"Pt{(npair + n) % 3}"], [C.bk(bo)])

            s_pair(0)
            for n in range(64):
                if n + 1 < 64:
                    s_pair(n + 1)
                pv_pair(n)
            npair += 64
            a0 = S_pairs[npair % 3][0]
            pt = Pt[npair % 3]
            mm(C.bank(a0)[0:16, :], Kb[hb][0:96, 16384:16400], Qb[hb][0:96, qs], True, True, kKall + [kQ], [C.bk(a0)])
            act(pt[0:16, 0:512], C.bank(a0)[0:16, :], AF.Exp, [C.bk(a0)], [f"Pt{npair % 3}"])
            mm(C.bank(bo)[:, :], Vb[hb][0:16, 128, :], pt[0:16, 0:512], False, True, kVall + [f"Pt{npair % 3}"], [C.bk(bo)])
            npair += 1
            act(Osb[:], C.bank(bo)[:, :], AF.Copy, [C.bk(bo)], ["Osb"])
            bd = S_pairs[npair % 3][1]
            mm(C.bank(bd)[0:64, :], C.shsel[:, :], Osb[:], True, True, ["shsel", "Osb"], [C.bk(bd)])
            op("dve", lambda e, bd=bd: e.reciprocal(rd[:], C.bank(bd)[0:64, :]), [C.bk(bd)], ["rd"])
            tt("pool", ya[:], Osb[0:64, :], rd[:], ALU.mult, ["Osb", "rd"], ["ya"])
            tt("pool", yast[nfin % 2][:], ya[:], gt[nfin % 2][:], ALU.mult, ["ya", f"gt{nfin % 2}"], [f"yast{nfin % 2}"])
            C.st(C.Yad[64 * h:64 * h + 64, qs], yast[nfin % 2][:], f"Syast{nfin % 2}", [f"yast{nfin % 2}"])
            nfin += 1
    C.dump("Yad", C.Yad, [512, 4096], "Syast0", BF16)
    P.release(mA)


def phase_out(C):
    P, D, nc = C.P, C.D, C.nc
    sb, op, mm, tr, act, tt, ts, stt, cp, ms, ld = P.sb, C.op, C.mm, C.tr, C.act, C.tt, C.ts, C.stt, C.cp, C.ms, C.ld
    mO = P.mark()
    Y8 = sb("Y8all", [128, 32, 512], BF16)
    for g4 in range(4):
        ld(Y8[:, 8 * g4:8 * g4 + 8, :], C.Y8d.rearrange("g p c -> p g c")[:, 8 * g4:8 * g4 + 8, :], [f"Y8all{g4}"])
    sel = sb("sel", [128, 64, 128], BF16)
    ld(sel[:], D["sel"][:, :, :], ["sel"])
    wglub = sb("wglub", [128, 4, 1024], BF16)
    woutb = sb("woutb", [128, 8, 1024], BF16)
    bglu = sb("bglu", [128, 8], F32)
    onw = sb("onw", [128, 8], F32)
    postw = sb("postw", [128, 1024], F32)
    ld(bglu[:], D["bglu"][:, :], ["bglu"])
    ld(onw[:], D["on_w"][:, :], ["onw"])
    ld(postw[:], D["post_w"][:, :], ["postw"])
    wst = [sb(f"owst{i}", [128, 1024], F32) for i in range(2)]
    for k in range(4):
        ld(wst[k % 2][:], D["wglu_t"][:, k, :], [f"owst{k % 2}"])
        cp("dve", wglub[:, k, :], wst[k % 2][:], [f"owst{k % 2}"], ["wglub"])
    for k in range(8):
        ld(wst[k % 2][:], D["wout_t"][:, k, :], [f"owst{k % 2}"])
        act(woutb[:, k, :], wst[k % 2][:], AF.Identity, [f"owst{k % 2}", "onw"], ["woutb"], scale=onw[:, k:k + 1])
    Yat = sb("Yat", [128, 4, 512], BF16)
    Gst = sb("Gst", [128, 4, 512], BF16)
    yg = sb("yg", [128, 4, 512], BF16)
    ysT = sb("ysT", [128, 4, 512], BF16)
    ga = sb("ga", [128, 512], F32)
    gg = sb("gg", [128, 512], F32)
    sqa = sb("sqa", [128, 4, 512], F32)
    sqs = sb("sqs", [128, 4, 512], F32)
    rst = sb("rst", [128, 4], F32)
    hx = [sb(f"hx{i}", [128, 1024], F32) for i in range(2)]
    yo = sb("yo", [128, 1024], F32)
    yo2 = sb("yo2", [128, 1024], F32)
    ojunk = sb("ojunk", [128, 1024], F32)
    ost = [sb(f"ost{i}", [128, 1024], F32) for i in range(2)]
    xloc4 = D["x_loc"].rearrange("(t c i) d -> t c i d", c=128, i=8)
    out4 = C.out.rearrange("(t c i) d -> t c i d", c=128, i=8)
    Yad_p = C.Yad.rearrange("(m p) c -> p m c", p=128)
    Gsd_p = C.Gsd.rearrange("(m p) c -> p m c", p=128)
    nb_ = 0
    for i in range(8):
        c0 = 512 * i
        ld(Yat[:], Yad_p[:, :, c0:c0 + 512], ["Yat"])
        ld(Gst[:], Gsd_p[:, :, c0:c0 + 512], ["Gst"])
        for m in range(4):
            bb = C.nb()
            for g8 in range(8):
                mm(C.bank(bb)[:, :], sel[:, 8 * i + g8, :], Y8[:, 8 * m + g8, :], g8 == 0, g8 == 7, ["sel", f"Y8all{m}"], [C.bk(bb)])
            act(yg[:, m, :], C.bank(bb)[:, :], AF.Gelu_apprx_tanh, [C.bk(bb)], ["yg"])
        for fb in range(4):
            ba, bg = C.nb(), C.nb()
            for (bb, f0) in ((ba, fb), (bg, fb + 4)):
                for m in range(4):
                    mm(C.bank(bb)[:, :], wglub[:, m, 128 * f0:128 * f0 + 128], yg[:, m, :], m == 0, m == 3, ["wglub", "yg"], [C.bk(bb)])
            act(ga[:], C.bank(ba)[:, :], AF.Identity, [C.bk(ba), "bglu"], ["ga"], bias=bglu[:, fb:fb + 1])
            act(gg[:], C.bank(bg)[:, :], AF.Sigmoid, [C.bk(bg), "bglu"], ["gg"], bias=bglu[:, fb + 4:fb + 5])
            tt("dve", ga[:], ga[:], gg[:], ALU.mult, ["ga", "gg"], ["ga"])
            tt("dve", ysT[:, fb, :], ga[:], Gst[:, fb, :], ALU.mult, ["ga", "Gst"], ["ysT"])
        tt("pool", sqa[:], Yat[:], Yat[:], ALU.mult, ["Yat"], ["sqa"])
        tt("pool", sqs[:], ysT[:], ysT[:], ALU.mult, ["ysT"], ["sqs"])
        for T in range(4):
            ts_ = slice(128 * T, 128 * T + 128)
            hxb = hx[nb_ % 2]
            ld(hxb[:], xloc4[T, :, i, :], [f"hx{nb_ % 2}"])
            bst = C.nb()
            for (col, sq_) in ((0, sqa), (1, sqs)):
                for m in range(4):
                    mm(C.bank(bst)[:, col:col + 1], sq_[:, m, ts_], C.ones_f[:, 0:1], m == 0, m == 3, ["sqa", "sqs", "ones_f"], [C.bk(bst)])
            act(rst[:, 0:2], C.bank(bst)[:, 0:2], AF.Sqrt, [C.bk(bst)], ["rst"], scale=1.0 / 512.0, bias=EPS)
            op("dve", lambda e: e.reciprocal(rst[:, 0:2], rst[:, 0:2]), ["rst"], ["rst"])
            pa = [C.nb(), C.nb()]
            ps_ = [C.nb(), C.nb()]
            for nh in range(2):
                for m in range(4):
                    mm(C.bank(pa[nh])[:, :], Yat[:, m, ts_], woutb[:, m, 512 * nh:512 * nh + 512], m == 0, m == 3, ["Yat", "woutb"], [C.bk(pa[nh])])
                for m in range(4):
                    mm(C.bank(ps_[nh])[:, :], ysT[:, m, ts_], woutb[:, 4 + m, 512 * nh:512 * nh + 512], m == 0, m == 3, ["ysT", "woutb"], [C.bk(ps_[nh])])
            for nh in range(2):
                cs = slice(512 * nh, 512 * nh + 512)
                act(yo[:, cs], C.bank(pa[nh])[:, :], AF.Identity, [C.bk(pa[nh]), "rst"], ["yo"], scale=rst[:, 0:1])
                stt(yo[:, cs], C.bank(ps_[nh])[:, :], rst[:, 1:2], yo[:, cs], ALU.mult, ALU.add, [C.bk(ps_[nh]), "rst", "yo"], ["yo"])
            stt(ojunk[:], yo[:], 1.0, yo[:], ALU.mult, ALU.mult, ["yo"], ["ojunk", "rst"], accum=rst[:, 2:3])
            act(rst[:, 2:3], rst[:, 2:3], AF.Sqrt, ["rst"], ["rst"], scale=1.0 / 1024.0, bias=EPS)
            op("dve", lambda e: e.reciprocal(rst[:, 2:3], rst[:, 2:3]), ["rst"], ["rst"])
            stt(yo2[:], yo[:], rst[:, 2:3], postw[:], ALU.mult, ALU.mult, ["yo", "rst", "postw"], ["yo2"])
            o_ = ost[nb_ % 2]
            tt("pool", o_[:], yo2[:], hxb[:], ALU.add, ["yo2", f"hx{nb_ % 2}"], [f"ost{nb_ % 2}"])
            C.st(out4[T, :, i, :], o_[:], f"Sost{nb_ % 2}", [f"ost{nb_ % 2}"])
            nb_ += 1
    P.release(mO)


_CACHE = {}


def kernel(**inputs):
    shared = _prep_shared(inputs)
    maps = [_prep_core(inputs, shared, c) for c in range(8)]
    if "nc" not in _CACHE:
        _CACHE["nc"] = build(maps[0])[0]
    nc = _CACHE["nc"]
    x = np.asarray(inputs["x"])
    out = np.empty(x.shape, np.float32)
    if SINGLE_LAUNCH:
        res = run_bass_kernel_spmd(nc, maps, core_ids=list(range(8)))
        results = res.results
    else:
        results = []
        for c in range(8):
            r = run_bass_kernel_spmd(nc, [maps[c]], core_ids=[0])
            results.append({"out": np.array(r.results[0]["out"])})
            maps[c] = None
    for c in range(8):
        b, q = c // 4, c % 4
        out[b, 4096 * q:4096 * (q + 1)] = results[c]["out"]
    return out
```
